# Optimizing a Trainium2 kernel written in Bass

```python
import math
import jax, jax.numpy as jnp
from jax import lax
import numpy as np


D_MODEL = 1024
BATCH = 8
SEQ = 8192
DEPTH = 1

PLE_DIM = 256
D_FF = 2816
RWKV_HEADS = 8
RWKV_HEAD_DIM = 64
RWKV_WIDTH = RWKV_HEADS * RWKV_HEAD_DIM
DECAY_LORA = 64
AAA_LORA = 64
GATE_LORA = 128
RWKV_GN_EPS = 64e-5
DECAY_SCALE = math.exp(-0.5)
RET_HEADS = 4
RET_QK_DIM = 128
RET_V_DIM = 256
RET_QK_WIDTH = RET_HEADS * RET_QK_DIM
RET_V_WIDTH = RET_HEADS * RET_V_DIM
RET_CHUNK = 128
ROPE_BASE = 10000.0
RWKV_IN = 3 * RWKV_WIDTH + DECAY_LORA + AAA_LORA + GATE_LORA
RET_IN = 2 * RET_QK_WIDTH + 2 * RET_V_WIDTH
GATE_IN = 2 * D_MODEL
MIX_IN = RWKV_IN + RET_IN + GATE_IN
DN_ALPHA = (2 * DEPTH) ** 0.25
DN_BETA = (8 * DEPTH) ** -0.25
LN_EPS = 1e-5

kernel_name = "hybrid_rwkv7_retention_macaron_deepnorm"


def layer_norm(x, g, b, eps=LN_EPS):
    xf = x.astype(jnp.float32)
    mu = jnp.mean(xf, -1, keepdims=True)
    var = jnp.mean(jnp.square(xf - mu), -1, keepdims=True)
    return ((xf - mu) * lax.rsqrt(var + eps) * g + b).astype(x.dtype)


def head_norm(y, eps):
    mu = jnp.mean(y, -1, keepdims=True)
    var = jnp.mean(jnp.square(y - mu), -1, keepdims=True)
    return (y - mu) * lax.rsqrt(var + eps)


def swiglu(x, w_in, w_out):
    gate, up = jnp.split(x @ w_in, 2, axis=-1)
    return (jax.nn.silu(gate) * up) @ w_out


def token_shift(z):
    return jnp.pad(z[:, :-1], ((0, 0), (1, 0), (0, 0)))


def rwkv_step(state, inp):
    r_t, w_t, k_t, v_t, kk_t, a_t = inp
    sa = jnp.einsum('bhvk,bhk->bhv', state, kk_t)
    state = (state * w_t[:, :, None, :]
             - sa[..., None] * (kk_t * a_t)[:, :, None, :]
             + v_t[..., None] * k_t[:, :, None, :])
    return state, jnp.einsum('bhvk,bhk->bhv', state, r_t)


def rwkv7_mix(z, mu, w0, w_up, a0, a_up, g_up, k_k, k_a, r_k, gn_g, gn_b):
    B, S, _ = z.shape
    H, N, W = RWKV_HEADS, RWKV_HEAD_DIM, RWKV_WIDTH
    f32 = jnp.float32
    z = z + (token_shift(z) - z) * mu
    r, k, v, dw, da, dg = jnp.split(
        z, [W, 2 * W, 3 * W, 3 * W + DECAY_LORA, 3 * W + DECAY_LORA + AAA_LORA], axis=-1)
    log_w = -DECAY_SCALE * jax.nn.sigmoid((w0 + jnp.tanh(dw) @ w_up).astype(f32))
    a = jax.nn.sigmoid((a0 + da @ a_up).astype(f32))
    g = (jax.nn.sigmoid(dg) @ g_up).astype(f32)
    kk = (k * k_k).astype(f32).reshape(B, S, H, N)
    kk = kk * lax.rsqrt(jnp.maximum(jnp.sum(jnp.square(kk), -1, keepdims=True), 1e-24))
    k = (k.astype(f32) * (1.0 + (a - 1.0) * k_a)).reshape(B, S, H, N)
    r = r.astype(f32).reshape(B, S, H, N)
    v = v.astype(f32).reshape(B, S, H, N)
    a = a.reshape(B, S, H, N)
    w = jnp.exp(log_w).reshape(B, S, H, N)
    tm = lambda t: jnp.moveaxis(t, 1, 0)
    state0 = jnp.zeros((B, H, N, N), f32)
    _, y = lax.scan(rwkv_step, state0, (tm(r), tm(w), tm(k), tm(v), tm(kk), tm(a)))
    y = jnp.moveaxis(y, 0, 1)
    y = head_norm(y, RWKV_GN_EPS).reshape(B, S, W) * gn_g + gn_b
    bonus = jnp.sum(r * k * r_k, -1, keepdims=True) * v
    y = y + bonus.reshape(B, S, W)
    return (y * g).astype(z.dtype)


def rotary(x, positions):
    half = x.shape[-1] // 2
    inv_freq = ROPE_BASE ** (-jnp.arange(half, dtype=jnp.float32) / half)
    ang = positions.astype(jnp.float32)[..., None] * inv_freq
    cos = jnp.cos(ang)[:, :, None, :]
    sin = jnp.sin(ang)[:, :, None, :]
    x1, x2 = x[..., :half], x[..., half:]
    return jnp.concatenate([x1 * cos - x2 * sin, x2 * cos + x1 * sin], -1)


def retention_mix(z, positions):
    B, S, _ = z.shape
    H, Dk, Dv, C = RET_HEADS, RET_QK_DIM, RET_V_DIM, RET_CHUNK
    nC = S // C
    f32 = jnp.float32
    q, k, v, g = jnp.split(z, [RET_QK_WIDTH, 2 * RET_QK_WIDTH, 2 * RET_QK_WIDTH + RET_V_WIDTH], axis=-1)
    q = rotary(q.astype(f32).reshape(B, S, H, Dk), positions)
    k = rotary(k.astype(f32).reshape(B, S, H, Dk), positions) * (Dk ** -0.5)
    v = v.astype(f32).reshape(B, S, H, Dv)
    log_gamma = jnp.log(1.0 - jnp.exp2(-5.0 - jnp.arange(H, dtype=f32)))
    idx = jnp.arange(C, dtype=f32)
    rel = idx[:, None] - idx[None, :]
    decay_mask = jnp.where(rel >= 0, jnp.exp(log_gamma[:, None, None] * jnp.maximum(rel, 0.0)), 0.0)
    xi = jnp.exp(log_gamma[:, None] * (idx + 1.0))
    zeta = jnp.exp(log_gamma[:, None] * (C - 1.0 - idx))
    chunk_decay = jnp.exp(log_gamma * C)

    def to_chunks(t):
        return t.reshape(B, nC, C, H, t.shape[-1]).transpose(1, 0, 3, 2, 4)

    def step(R, qkv):
        qc, kc, vc = qkv
        scores = jnp.einsum('bhnd,bhmd->bhnm', qc, kc) * decay_mask
        inner = jnp.einsum('bhnm,bhme->bhne', scores, vc)
        cross = jnp.einsum('bhnd,bhde->bhne', qc, R) * xi[None, :, :, None]
        R = R * chunk_decay[None, :, None, None] + jnp.einsum('bhmd,bhme->bhde', kc * zeta[None, :, :, None], vc)
        return R, inner + cross

    R0 = jnp.zeros((B, H, Dk, Dv), f32)
    _, y = lax.scan(step, R0, (to_chunks(q), to_chunks(k), to_chunks(v)))
    y = y.transpose(1, 0, 3, 2, 4).reshape(B, S, H, Dv)
    y = head_norm(y, LN_EPS).reshape(B, S, RET_V_WIDTH)
    return (jax.nn.silu(g.astype(f32)) * y).astype(z.dtype)


def hybrid_mix(h, positions, w_in, mu, w0, w_up, a0, a_up, g_up, k_k, k_a, r_k, gn_g, gn_b,
               w_branch_rwkv, w_branch_ret, w_out):
    z = h @ w_in
    z_rwkv, z_ret, z_gate = jnp.split(z, [RWKV_IN, RWKV_IN + RET_IN], axis=-1)
    y_rwkv = rwkv7_mix(z_rwkv, mu, w0, w_up, a0, a_up, g_up, k_k, k_a, r_k, gn_g, gn_b) @ w_branch_rwkv
    y_ret = retention_mix(z_ret, positions) @ w_branch_ret
    gate_rwkv, gate_ret = jnp.split(jax.nn.sigmoid(z_gate), 2, axis=-1)
    return (gate_rwkv * y_rwkv + gate_ret * y_ret) @ w_out


def setup_inputs(seed: int = 0) -> dict:
    key = jax.random.key(seed)
    ks = iter(jax.random.split(key, 40))
    f32 = jnp.float32
    nrm = lambda shape, s: jax.random.normal(next(ks), shape, f32) * s
    L, D, F = DEPTH, D_MODEL, D_FF
    x = nrm((BATCH, SEQ, D), 1.0)
    p = nrm((DEPTH, BATCH, SEQ, PLE_DIM), 1.0)
    start = jax.random.randint(next(ks), (BATCH, 1), 0, 4096, dtype=jnp.int32)
    positions = (start + jnp.arange(SEQ, dtype=jnp.int32)[None, :]).astype(jnp.int32)
    return {
        "x": x,
        "p": p,
        "positions": positions,
        "ln1_g": 1.0 + nrm((L, D), 0.1),
        "ln1_b": nrm((L, D), 0.01),
        "ffn1_w_in": nrm((L, D, 2 * F), D ** -0.5),
        "ffn1_w_out": nrm((L, F, D), F ** -0.5 * DN_BETA),
        "w_mix_in": nrm((L, D, MIX_IN), D ** -0.5),
        "rwkv_mu": jax.random.uniform(next(ks), (L, RWKV_IN), f32),
        "rwkv_w0": nrm((L, RWKV_WIDTH), 1.0),
        "rwkv_w_up": nrm((L, DECAY_LORA, RWKV_WIDTH), 0.5 * DECAY_LORA ** -0.5),
        "rwkv_a0": nrm((L, RWKV_WIDTH), 0.5),
        "rwkv_a_up": nrm((L, AAA_LORA, RWKV_WIDTH), 0.5 * AAA_LORA ** -0.5),
        "rwkv_g_up": nrm((L, GATE_LORA, RWKV_WIDTH), GATE_LORA ** -0.5),
        "rwkv_k_k": 0.85 + nrm((L, RWKV_WIDTH), 0.1),
        "rwkv_k_a": 1.0 + nrm((L, RWKV_WIDTH), 0.1),
        "rwkv_r_k": nrm((L, RWKV_HEADS, RWKV_HEAD_DIM), 0.1),
        "rwkv_gn_g": 1.0 + nrm((L, RWKV_WIDTH), 0.1),
        "rwkv_gn_b": nrm((L, RWKV_WIDTH), 0.01),
        "w_branch_rwkv": nrm((L, RWKV_WIDTH, D), RWKV_WIDTH ** -0.5),
        "w_branch_ret": nrm((L, RET_V_WIDTH, D), RET_V_WIDTH ** -0.5),
        "w_mix_out": nrm((L, D, D), D ** -0.5 * DN_BETA),
        "ln2_g": 1.0 + nrm((L, D), 0.1),
        "ln2_b": nrm((L, D), 0.01),
        "ffn2_w_in": nrm((L, D, 2 * F), D ** -0.5),
        "ffn2_w_out": nrm((L, F, D), F ** -0.5 * DN_BETA),
        "ln3_g": 1.0 + nrm((L, D), 0.1),
        "ln3_b": nrm((L, D), 0.01),
        "ple_w_proj": nrm((L, PLE_DIM, D), PLE_DIM ** -0.5),
        "ple_w_gate": nrm((L, D, D), D ** -0.5),
    }


def reference(x, p, positions, ln1_g, ln1_b, ffn1_w_in, ffn1_w_out, w_mix_in, rwkv_mu, rwkv_w0, rwkv_w_up,
              rwkv_a0, rwkv_a_up, rwkv_g_up, rwkv_k_k, rwkv_k_a, rwkv_r_k, rwkv_gn_g, rwkv_gn_b,
              w_branch_rwkv, w_branch_ret, w_mix_out, ln2_g, ln2_b, ffn2_w_in, ffn2_w_out, ln3_g, ln3_b,
              ple_w_proj, ple_w_gate):
    h = x
    for i in range(DEPTH):
        h = layer_norm(DN_ALPHA * h + 0.5 * swiglu(h, ffn1_w_in[i], ffn1_w_out[i]), ln1_g[i], ln1_b[i])
        mix = hybrid_mix(h, positions, w_mix_in[i], rwkv_mu[i], rwkv_w0[i], rwkv_w_up[i], rwkv_a0[i],
                         rwkv_a_up[i], rwkv_g_up[i], rwkv_k_k[i], rwkv_k_a[i], rwkv_r_k[i], rwkv_gn_g[i],
                         rwkv_gn_b[i], w_branch_rwkv[i], w_branch_ret[i], w_mix_out[i])
        h = layer_norm(DN_ALPHA * h + mix, ln2_g[i], ln2_b[i])
        h = layer_norm(DN_ALPHA * h + 0.5 * swiglu(h, ffn2_w_in[i], ffn2_w_out[i]), ln3_g[i], ln3_b[i])
        h = h + jax.nn.sigmoid(h @ ple_w_gate[i]) * (p[i] @ ple_w_proj[i])
    return h
```

```python
import math
import os
from contextlib import ExitStack
import numpy as np
import concourse.bass as bass
import concourse.mybir as mybir
from concourse.bass_utils import run_bass_kernel_spmd

F32 = mybir.dt.float32
BF16 = mybir.dt.bfloat16
I32 = mybir.dt.int32
AF = mybir.ActivationFunctionType
ALU = mybir.AluOpType

DSER = int(os.environ.get('K_DSER', '1000000'))
NPOOL = 64
SAME_DIST = 3


class Buf:
    __slots__ = ("w", "r", "name")

    def __init__(self, name=""):
        self.w = []
        self.r = []
        self.name = name


class Op:
    __slots__ = ("eng", "fn", "waits", "dma", "signal", "epoch", "cnt")

    def __init__(self, eng, fn, epoch):
        self.eng = eng
        self.fn = fn
        self.waits = []
        self.dma = None
        self.signal = False
        self.epoch = epoch
        self.cnt = 0


class Sched:
    ENGS = ("pe", "act", "dve", "pool", "sp")

    def __init__(self):
        self.ops = {e: [] for e in self.ENGS}
        self.waited = {e: {} for e in self.ENGS}
        self.waited_d = {e: {} for e in self.ENGS}
        self.ndma = 0
        self.epoch = 0
        self.nepoch = 1

    def next_epoch(self):
        self.epoch += 1
        self.nepoch = self.epoch + 1

    def add(self, eng, fn, reads=(), writes=(), dma=False):
        ops = self.ops[eng]
        idx = len(ops)
        op = Op(eng, fn, self.epoch)
        deps = []
        for b in reads:
            deps.extend(b.w)
        for b in writes:
            deps.extend(b.w)
            deps.extend(b.r)
        if dma:
            did = self.ndma
            self.ndma += 1
            op.dma = did
            if did >= NPOOL:
                deps.append(("d", did - NPOOL))
            if did >= DSER:
                deps.append(("d", did - DSER))
        wt = self.waited[eng]
        wd = self.waited_d[eng]
        for a in deps:
            if a[0] == "e":
                _, e2, i2 = a
                if e2 == eng:
                    if eng == "pe" or eng == "sp":
                        continue
                    if idx - i2 > SAME_DIST:
                        continue
                if wt.get(e2, -1) >= i2:
                    continue
                wt[e2] = i2
                op.waits.append(a)
                self.ops[e2][i2].signal = True
            else:
                did2 = a[1]
                k = did2 % NPOOL
                tgt = 16 * (did2 // NPOOL + 1)
                if wd.get(k, 0) >= tgt:
                    continue
                wd[k] = tgt
                op.waits.append(a)
        acc = ("d", op.dma) if dma else ("e", eng, idx)
        for b in writes:
            b.w = [acc]
            b.r = []
        for b in reads:
            if b in writes:
                continue
            if acc[0] == "e":
                b.r = [x for x in b.r if not (x[0] == "e" and x[1] == eng)]
            b.r.append(acc)
        ops.append(op)
        return op

    def emit(self, nc, block, sems, dsems):
        for e in self.ENGS:
            c = {}
            for op in self.ops[e]:
                if op.signal:
                    c[op.epoch] = c.get(op.epoch, 0) + 1
                op.cnt = c.get(op.epoch, 0)
        ops_all = self.ops
        import os
        if os.environ.get("K_DUMP") == "1":
            for e in self.ENGS:
                for i, op in enumerate(self.ops[e]):
                    ws = []
                    for a in op.waits:
                        if a[0] == "e":
                            o2 = self.ops[a[1]][a[2]]
                            ws.append(f"{a[1]}#{a[2]}(ep{o2.epoch}>={o2.cnt})")
                        else:
                            ws.append(f"dma{a[1]}")
                    print(e, i, "ep", op.epoch, "dma" if op.dma is not None else "", op.dma if op.dma is not None else "", "sig" if op.signal else "", op.cnt, "waits:", " ".join(ws), "NOP" if op.fn is None else "")

        def body(eng_name):
            def f(h):
                for op in ops_all[eng_name]:
                    for a in op.waits:
                        if a[0] == "e":
                            o2 = ops_all[a[1]][a[2]]
                            h.wait_ge(sems[a[1]][o2.epoch], o2.cnt)
                        else:
                            did2 = a[1]
                            h.wait_ge(dsems[did2 % NPOOL], 16 * (did2 // NPOOL + 1))
                    if op.fn is None:
                        continue
                    ins = op.fn(h)
                    if op.dma is not None:
                        ins.then_inc(dsems[op.dma % NPOOL], 16)
                    elif op.signal:
                        ins.then_inc(sems[eng_name][op.epoch], 1)
            return f

        block.tensor(body("pe"))
        block.scalar(body("act"))
        block.vector(body("dve"))
        block.gpsimd(body("pool"))
        block.sync(body("sp"))


class Tile:
    def __init__(self, ap, bufs):
        self.ap = ap
        self.bufs = bufs

    def __getitem__(self, k):
        return self.ap[k]

    @property
    def b(self):
        return self.bufs


class Arena:
    def __init__(self, ap, n):
        self.base = ap
        self.n = n
        self.sp = 0
        self.stack = []
        self.hist = []
        self.peak = 0

    def alloc(self, nelem, shape=None, parts=128, nsub=1, name=""):
        off = self.sp
        end = off + nelem
        assert end <= self.n, f"arena overflow {name}: {end} > {self.n}"
        self.sp = end
        self.peak = max(self.peak, end)
        bufs = [Buf(name + str(i)) for i in range(nsub)]
        inh = []
        for (o, e, acc) in self.hist:
            if o < end and e > off:
                inh.extend(acc)
        for b in bufs:
            b.r = list(inh)
        self.hist = [(o, e, acc) for (o, e, acc) in self.hist if not (o >= off and e <= end)]
        self.stack.append((off, end, bufs))
        ap = self.base[0:parts, off:end]
        if shape is not None and len(shape) > 1:
            names = " ".join(f"d{i}" for i in range(len(shape)))
            kw = {f"d{i}": s for i, s in enumerate(shape)}
            ap = ap.rearrange(f"p ({names}) -> p {names}", **kw)
        return Tile(ap, bufs)

    def mark(self):
        return (self.sp, len(self.stack))

    def release(self, m):
        sp, ns = m
        while len(self.stack) > ns:
            off, end, bufs = self.stack.pop()
            acc = []
            for b in bufs:
                acc.extend(b.w)
                acc.extend(b.r)
            self.hist.append((off, end, acc))
        self.sp = sp

D = 1024; FF = 2816; SEQ = 8192; NB_ = 8
TT = 256
NSLOT = 4
BLK = 4096
RW_IN = 1792; RET_IN_ = 3072
ALPHA = 2.0 ** 0.25
DEC = math.exp(-0.5)
GAM = [1.0 - 2.0 ** (-5.0 - h) for h in range(4)]


def _blk_fm(W, cols):
    K = W.shape[0]; KC = K // 128
    sub = W[:, cols].reshape(KC, 128, len(cols)).transpose(1, 0, 2).reshape(128, -1)
    out = np.zeros((128, BLK), np.float32); out[:, :sub.shape[1]] = sub
    return out


def _blk_heads(W, cols):
    sub = W[:, cols].reshape(8, 64, len(cols)).transpose(1, 0, 2).reshape(64, -1)
    out = np.zeros((128, BLK), np.float32); out[:64, :sub.shape[1]] = sub
    return out


def _ffn_blocks(w_in, w_out):
    bl = []
    for g in range(11):
        cols = np.concatenate([np.arange(2 * g * 128, (2 * g + 2) * 128), FF + np.arange(2 * g * 128, (2 * g + 2) * 128)])
        bl.append(_blk_fm(w_in, cols))
    for j in range(8):
        bl.append(_blk_fm(w_out, np.arange(j * 128, (j + 1) * 128)))
    return bl


def host_prep(I):
    g = lambda k: np.asarray(I[k], np.float32)[0]
    bl = []
    bl += _ffn_blocks(g("ffn1_w_in"), g("ffn1_w_out"))
    wm = g("w_mix_in")
    R0 = RW_IN
    bl.append(_blk_fm(wm, R0 + np.arange(0, 512)))
    bl.append(_blk_fm(wm, R0 + np.arange(512, 1024)))
    bl.append(_blk_fm(wm, R0 + 1024 + np.arange(0, 512)))
    bl.append(_blk_fm(wm, R0 + 1024 + np.arange(512, 1024)))
    bl.append(_blk_fm(wm, R0 + 2048 + np.arange(0, 512)))
    bl.append(_blk_fm(wm, R0 + 2048 + np.arange(512, 1024)))
    bl.append(_blk_fm(wm, np.arange(1536, 1792)))
    bl.append(_blk_fm(wm, np.arange(0, 512)))
    bl.append(_blk_fm(wm, np.arange(512, 1024)))
    bl.append(_blk_fm(wm, np.arange(1024, 1536)))
    GB = RW_IN + RET_IN_
    wbr = g("w_branch_rwkv"); wbt = g("w_branch_ret")
    for hb in range(2):
        bl.append(_blk_heads(wbr, np.arange(hb * 512, (hb + 1) * 512)))
        bl.append(_blk_fm(wbt, np.arange(hb * 512, (hb + 1) * 512)))
        for jj in (2 * hb, 2 * hb + 1):
            cols = np.concatenate([GB + np.arange(2 * jj * 128, (2 * jj + 2) * 128), GB + 1024 + np.arange(2 * jj * 128, (2 * jj + 2) * 128)])
            bl.append(_blk_fm(wm, cols))
    wo = g("w_mix_out")
    bl.append(_blk_fm(wo, np.arange(0, 512))); bl.append(_blk_fm(wo, np.arange(512, 1024)))
    bl += _ffn_blocks(g("ffn2_w_in"), g("ffn2_w_out"))
    pg = g("ple_w_gate")
    bl.append(_blk_fm(pg, np.arange(0, 512))); bl.append(_blk_fm(pg, np.arange(512, 1024)))
    bl.append(_blk_fm(g("ple_w_proj"), np.arange(0, 1024)))
    wall = np.stack(bl)
    assert wall.shape[0] == NBLK
    V = np.zeros((128, NVEC), np.float32)
    def put(name, arr):
        o, n = VOFF[name]; arr = np.asarray(arr, np.float32)
        V[:arr.shape[0], o:o + n] = arr.reshape(arr.shape[0], n)
    for nm in ("ln1_g", "ln1_b", "ln2_g", "ln2_b", "ln3_g", "ln3_b"):
        put(nm, g(nm).reshape(8, 128).T)
    mu = g("rwkv_mu")
    put("mu_r", mu[0:512].reshape(8, 64).T); put("mu_k", mu[512:1024].reshape(8, 64).T); put("mu_v", mu[1024:1536].reshape(8, 64).T)
    put("mu_dw", mu[1536:1600].reshape(64, 1)); put("mu_da", mu[1600:1664].reshape(64, 1)); put("mu_dg", mu[1664:1792].reshape(128, 1))
    for nm, k in (("k_k", "rwkv_k_k"), ("k_a", "rwkv_k_a"), ("gn_g", "rwkv_gn_g"), ("gn_b", "rwkv_gn_b")):
        put(nm, g(k).reshape(8, 64).T)
    put("r_k", g("rwkv_r_k").reshape(8, 64).T)
    lora = np.zeros((128, 1536), np.float32)
    lora[0:64, 0:512] = g("rwkv_w_up"); lora[64, 0:512] = g("rwkv_w0")
    lora[0:64, 512:1024] = g("rwkv_a_up"); lora[64, 512:1024] = g("rwkv_a0")
    lora[:, 1024:1536] = g("rwkv_g_up")
    return wall, V, lora


VOFF = {}
_o = 0
for _nm, _n in (("ln1_g", 8), ("ln1_b", 8), ("ln2_g", 8), ("ln2_b", 8), ("ln3_g", 8), ("ln3_b", 8),
                ("mu_r", 8), ("mu_k", 8), ("mu_v", 8), ("mu_dw", 1), ("mu_da", 1), ("mu_dg", 1),
                ("k_k", 8), ("k_a", 8), ("gn_g", 8), ("gn_b", 8), ("r_k", 8), ("omka", 8)):
    VOFF[_nm] = (_o, _n); _o += _n
NVEC = _o
NBLK = 61

COFF = {}
_o = 0
for _nm, _n in (("ident", 128), ("onesm", 128), ("ones64m", 64), ("ones64", 64), ("rot", 128), ("reset", 1024),
                ("msl4", 512), ("mask4", 512), ("rmask", 512), ("rxi", 512), ("rzeta", 512), ("invf", 1),
                ("eps_ln", 1), ("eps_gn", 1), ("eps_hn", 1), ("zero", 1)):
    COFF[_nm] = (_o, _n); _o += _n
NCON = _o


def make_consts():
    C = np.zeros((128, NCON), np.float32)
    def put(nm, a):
        o, n = COFF[nm]; C[:a.shape[0], o:o + n] = a.reshape(a.shape[0], n)
    put("ident", np.eye(128, dtype=np.float32))
    put("onesm", np.full((128, 128), 1.0 / 1024, np.float32))
    put("ones64m", np.full((64, 64), 1.0 / 64, np.float32))
    put("ones64", np.ones((64, 64), np.float32))
    rot = np.zeros((128, 128), np.float32)
    for m in range(64):
        rot[m + 64, m] = -1.0
        rot[m, m + 64] = 1.0
    put("rot", rot)
    rs = np.ones((64, 8, 128), np.float32); rs[:, :, 0] = 0; rs[:, :, 64] = 0
    put("reset", rs)
    t = np.arange(128)
    same = (t[:, None] // 64) == (t[None, :] // 64)
    sl = (same & (t[None, :] < t[:, None])).astype(np.float32)
    put("msl4", np.repeat(sl[:, None, :], 4, axis=1))
    stT = (same & (t[:, None] < t[None, :])).astype(np.float32)
    inT = (same & (t[:, None] <= t[None, :])).astype(np.float32)
    put("mask4", np.stack([stT, inT, stT, inT], axis=1))
    gam = np.array(GAM, np.float64)
    rel = t[None, :] - t[:, None]
    rm = np.where(rel[None] >= 0, gam[:, None, None] ** np.maximum(rel[None], 0), 0.0)
    put("rmask", rm.transpose(1, 0, 2).astype(np.float32))
    xi = gam[:, None] ** (t[None, :] + 1.0)
    put("rxi", np.broadcast_to(xi[None], (128, 4, 128)).astype(np.float32))
    ze = gam[:, None] ** (127.0 - t[None, :])
    put("rzeta", np.broadcast_to(ze.T[:, :, None], (128, 4, 128)).astype(np.float32))
    half = 64
    invf = (10000.0 ** (-np.arange(half, dtype=np.float32) / half)).astype(np.float32)
    put("invf", np.concatenate([invf, invf]).reshape(128, 1))
    put("eps_ln", np.full((128, 1), 1e-5 / (ALPHA * ALPHA), np.float32))
    put("eps_gn", np.full((128, 1), 64e-5, np.float32))
    put("eps_hn", np.full((128, 1), 1e-5, np.float32))
    return C


def build(n_tiles, stop=99):
    nc = bass.Bass("TRN2", target_bir_lowering=False)
    SL = n_tiles * TT
    x_d = nc.dram_tensor("x", [SL, D], F32, kind="ExternalInput").ap()
    p_d = nc.dram_tensor("p", [SL, 256], F32, kind="ExternalInput").ap()
    pos_d = nc.dram_tensor("pos", [1, SL], I32, kind="ExternalInput").ap()
    wall_d = nc.dram_tensor("wall", [NBLK, 128, BLK], F32, kind="ExternalInput").ap()
    vec_d = nc.dram_tensor("vecs", [128, NVEC], F32, kind="ExternalInput").ap()
    con_d = nc.dram_tensor("consts", [128, NCON], F32, kind="ExternalInput").ap()
    lora_d = nc.dram_tensor("lora", [128, 1536], F32, kind="ExternalInput").ap()
    out_d = nc.dram_tensor("out", [SL, D], F32, kind="ExternalOutput").ap()
    wsc_d = nc.dram_tensor("wsc", [NBLK, 128, BLK], BF16, kind="Internal").ap()

    S = Sched()
    NAR = 52000
    with ExitStack() as es:
        a32 = es.enter_context(nc.sbuf_tensor("arena", [128, NAR], F32))
        ps = es.enter_context(nc.psum_tensor("ps", [128, 4096], F32))
        import os
        nep = 1
        sems = {e: [es.enter_context(nc.semaphore(f"s_{e}_{i}")) for i in range(nep)] for e in ("pe", "act", "dve", "pool")}
        sems["sp"] = sems["pool"]
        dsems = [es.enter_context(nc.semaphore(f"d{k}")) for k in range(NPOOL)]
        block = es.enter_context(nc.Block())
        AR = Arena(a32, NAR)
        psb = [Buf(f"ps{k}") for k in range(8)]
        pctr = [0]

        def bank():
            k = pctr[0] % 8; pctr[0] += 1
            return k, ps[:, k * 512:(k + 1) * 512], [psb[k]]

        def al32(shape, parts=128, name=""):
            n = int(np.prod(shape))
            return AR.alloc(n, shape, parts, name=name)

        def al16(shape, parts=128, name=""):
            n = int(np.prod(shape))
            nw = (n + 1) // 2
            t = AR.alloc(nw, None, parts, name=name)
            ap = t.ap.bitcast(BF16)[:, 0:n]
            if len(shape) > 1:
                names = " ".join(f"d{i}" for i in range(len(shape)))
                kw = {f"d{i}": s for i, s in enumerate(shape)}
                ap = ap.rearrange(f"p ({names}) -> p {names}", **kw)
            return Tile(ap, t.bufs)

        def mm(out, lhsT, rhs, start, stop, rd, wr):
            S.add("pe", lambda h: h.matmul(out, lhsT=lhsT, rhs=rhs, start=start, stop=stop), reads=rd, writes=wr)

        def tr(out, in_, idn, rd, wr):
            S.add("pe", lambda h: h.transpose(out, in_, idn), reads=rd, writes=wr)

        def act(out, in_, func, rd, wr, bias=None, scale=1.0):
            if bias is None:
                S.add("act", lambda h: h.activation(out=out, in_=in_, func=func, scale=scale), reads=rd, writes=wr)
            else:
                S.add("act", lambda h: h.activation(out=out, in_=in_, func=func, bias=bias, scale=scale), reads=rd, writes=wr)

        def tt(eng, out, in0, in1, op, rd, wr):
            S.add(eng, lambda h: h.tensor_tensor(out=out, in0=in0, in1=in1, op=op), reads=rd, writes=wr)

        def ts(eng, out, in0, s1, s2, op0, op1, rd, wr):
            if s2 is None:
                S.add(eng, lambda h: h.tensor_scalar(out=out, in0=in0, scalar1=s1, scalar2=None, op0=op0), reads=rd, writes=wr)
            else:
                S.add(eng, lambda h: h.tensor_scalar(out=out, in0=in0, scalar1=s1, scalar2=s2, op0=op0, op1=op1), reads=rd, writes=wr)

        def stt(eng, out, in0, sc, in1, op0, op1, rd, wr):
            S.add(eng, lambda h: h.scalar_tensor_tensor(out=out, in0=in0, scalar=sc, in1=in1, op0=op0, op1=op1), reads=rd, writes=wr)

        def cp(eng, out, in_, rd, wr):
            if eng == "act":
                act(out, in_, AF.Copy, rd, wr)
            else:
                S.add(eng, lambda h: h.tensor_copy(out=out, in_=in_), reads=rd, writes=wr)

        def dma(eng, out, in_, rd, wr):
            S.add(eng, lambda h: h.dma_start(out=out, in_=in_), reads=rd, writes=wr, dma=True)

        def recip(out, in_, rd, wr):
            S.add("dve", lambda h: h.reciprocal(out=out, in_=in_), reads=rd, writes=wr)

        def scan(out, d0, d1, rd, wr):
            S.add("dve", lambda h: h.tensor_tensor_scan(out=out, data0=d0, data1=d1, initial=0.0, op0=ALU.mult, op1=ALU.add), reads=rd, writes=wr)

        def bnst(out, in_, rd, wr):
            S.add("dve", lambda h: h.bn_stats(out=out, in_=in_), reads=rd, writes=wr)

        def bnag(out, in_, rd, wr):
            S.add("dve", lambda h: h.bn_aggr(out=out, in_=in_), reads=rd, writes=wr)

        def mset(t_, val):
            S.add("dve", lambda h: h.memset(t_.ap, val), writes=t_.b)

        MUL, ADD, SUB, MAXOP = ALU.mult, ALU.add, ALU.subtract, ALU.max

        con = al32([NCON], name="con"); vec = al32([NVEC], name="vec")
        dma("sp", con.ap, con_d, [], con.b); dma("sp", vec.ap, vec_d, [], vec.b)
        C = lambda nm: con.ap[:, COFF[nm][0]:COFF[nm][0] + COFF[nm][1]]
        V = lambda nm: vec.ap[:, VOFF[nm][0]:VOFF[nm][0] + VOFF[nm][1]]
        identb = al16([128], name="identb"); ones64b = al16([64], parts=64, name="o64b")
        cp("dve", identb.ap, C("ident"), con.b, identb.b)
        cp("dve", ones64b.ap, C("ones64")[0:64], con.b, ones64b.b)
        ts("dve", V("omka")[0:64], V("k_a")[0:64], -1.0, 1.0, MUL, ADD, vec.b, vec.b)
        lorab = al16([1536], name="lorab")
        m0 = AR.mark()
        lst = al32([1536], name="lst")
        dma("sp", lst.ap, lora_d, [], lst.b)
        cp("dve", lorab.ap, lst.ap, lst.b, lorab.b)
        AR.release(m0)
        ring = [al16([BLK], name=f"ring{k}") for k in range(NSLOT)]
        NXB = int(os.environ.get("K_NXB", "2"))
        xts = [al32([TT // 128, D], name="xt") for _ in range(NXB)]
        pts = [al32([TT // 128, 256], name="pt") for _ in range(NXB)]
        posbs = [al32([TT], name="posb") for _ in range(NXB)]
        xt, pt, posb = xts[0], pts[0], posbs[0]
        Sst = al32([8, 64], parts=64, name="Sst")
        Rst = al32([4, 256], name="Rst"); Rb = al16([4, 256], name="Rb")
        car = al16([8, 3], parts=64, name="car")
        carl = al16([3], name="carl")
        tdw = al16([128], parts=65, name="tdw"); tda = al16([128], parts=65, name="tda")
        for t_ in (Sst, Rst, Rb, car, carl):
            mset(t_, 0.0)
        for t_ in (tdw, tda):
            mset(t_, 1.0)

        wbuf = [Buf(f"w{b}") for b in range(NBLK)]
        m0 = AR.mark()
        stg = [al32([BLK], name=f"stg{i}") for i in range(2)]
        stb = [al16([BLK], name=f"stb{i}") for i in range(2)]
        import os
        NOPRO = os.environ.get("K_NOPRO") == "1"
        NPRO = int(os.environ.get("K_NPRO", NBLK))
        for b in range(0, NPRO):
            s_ = stg[b % 2]; o_ = stb[b % 2]
            dma("sp", s_.ap, wall_d[b], [], s_.b)
            if os.environ.get("K_NOCAST") != "1":
                for q8 in range(8):
                    cp("dve", o_.ap[:, q8 * 512:(q8 + 1) * 512], s_.ap[:, q8 * 512:(q8 + 1) * 512], s_.b, o_.b)
            if os.environ.get("K_NOST") != "1":
                dma(os.environ.get("K_PSQ", "act"), wsc_d[b], o_.ap, o_.b, [wbuf[b]])
        AR.release(m0)

        total_blocks = int(os.environ.get("K_TB", n_tiles * NBLK))
        wst = {"next_load": 0, "next_use": 0}

        def w_load_upto(n):
            while wst["next_load"] < min(n, total_blocks if not NOPRO else 0):
                g_ = wst["next_load"]; b = g_ % NBLK; sl = ring[g_ % NSLOT]
                dma(os.environ.get("K_RQ", "sp"), sl.ap, wsc_d[b], [wbuf[b]], sl.b)
                wst["next_load"] += 1

        def w_get():
            g_ = wst["next_use"]; wst["next_use"] += 1
            assert g_ < wst["next_load"]
            return ring[g_ % NSLOT]

        def w_release():
            w_load_upto(wst["next_use"] + NSLOT)

        w_load_upto(NSLOT)

        NTB = TT // 128

        def ffn_ln(h, hT, lng, lnb, coef):
            m1 = AR.mark()
            hid = al16([22, TT], name="hid")
            sls = [al16([TT], name="sl") for _ in range(4)]
            for g_ in range(11):
                w = w_get(); wv = w.ap.rearrange("p (k n) -> p k n", n=512)
                pbs = []
                for c4 in range(4):
                    k_, pa, pbuf = bank()
                    for kc in range(8):
                        mm(pa[:, 0:TT], wv[:, kc, c4 * 128:(c4 + 1) * 128], hT.ap[:, kc, :], kc == 0, kc == 7, w.b + hT.b, pbuf)
                    pbs.append((pa, pbuf))
                w_release()
                for i in range(2):
                    sl = sls[(2 * g_ + i) % 4]
                    act(sl.ap, pbs[i][0][:, 0:TT], AF.Silu, pbs[i][1], sl.b)
                    tt("dve", hid.ap[:, 2 * g_ + i, :], pbs[2 + i][0][:, 0:TT], sl.ap, MUL, pbs[2 + i][1] + sl.b, hid.b)
            for j in range(8):
                w = w_get(); wv = w.ap[:, 0:22 * 128].rearrange("p (k n) -> p k n", n=128)
                k_, pa, pbuf = bank()
                for kc in range(22):
                    mm(pa[:, 0:TT], wv[:, kc, :], hid.ap[:, kc, :], kc == 0, kc == 21, w.b + hid.b, pbuf)
                w_release()
                stt("dve", h.ap[:, j, :], pa[:, 0:TT], coef, h.ap[:, j, :], MUL, ADD, pbuf + h.b, h.b)
            AR.release(m1)
            layer_norm(h, hT, lng, lnb)

        def layer_norm(h, hT, lng, lnb):
            m1 = AR.mark()
            k1, pm, pmb = bank(); k2, pq, pqb = bank()
            sqs = [al32([TT], name="sq") for _ in range(3)]
            for j in range(8):
                sq = sqs[j % 3]
                act(sq.ap, h.ap[:, j, :], AF.Square, h.b, sq.b)
                mm(pm[:, 0:TT], C("onesm"), h.ap[:, j, :], j == 0, j == 7, con.b + h.b, pmb)
                mm(pq[:, 0:TT], C("onesm"), sq.ap, j == 0, j == 7, con.b + sq.b, pqb)
            mean = al32([TT], name="mean"); var = al32([TT], name="var")
            cp("act", mean.ap, pm[:, 0:TT], pmb, mean.b)
            tt("dve", var.ap, mean.ap, mean.ap, MUL, mean.b, var.b)
            tt("dve", var.ap, pq[:, 0:TT], var.ap, SUB, pqb + var.b, var.b)
            act(var.ap, var.ap, AF.Sqrt, var.b, var.b, bias=C("eps_ln"))
            recip(var.ap, var.ap, var.b, var.b)
            t1s = [al32([TT], name="t1") for _ in range(3)]
            for j in range(8):
                t1 = t1s[j % 3]
                tt("dve", t1.ap, h.ap[:, j, :], mean.ap, SUB, h.b + mean.b, t1.b)
                tt("dve", t1.ap, t1.ap, var.ap, MUL, t1.b + var.b, t1.b)
                act(h.ap[:, j, :], t1.ap, AF.Identity, t1.b + vec.b, h.b, bias=V(lnb)[:, j:j + 1], scale=V(lng)[:, j:j + 1])
                act(hT.ap[:, j, :], t1.ap, AF.Identity, t1.b + vec.b, hT.b, bias=V(lnb)[:, j:j + 1], scale=V(lng)[:, j:j + 1])
            AR.release(m1)

        def retention(hT, yrT, tile_i):
            m1 = AR.mark()
            cos = al32([TT], name="cos"); sin = al32([TT], name="sin")
            m2 = AR.mark()
            posf = al32([TT], name="posf"); yv = al32([TT], name="yv"); ni = al32([TT], name="ni"); nf = al32([TT], name="nf")
            cp("dve", posf.ap, posb.ap.bitcast(I32), posb.b, posf.b)
            ts("dve", yv.ap, posf.ap, C("invf"), 1.0 / (2 * math.pi), MUL, MUL, posf.b + con.b, yv.b)
            for tab, off in ((sin, 0.0), (cos, 0.25)):
                if off:
                    ts("dve", yv.ap, yv.ap, off, None, ADD, None, yv.b, yv.b)
                cp("dve", ni.ap.bitcast(I32), yv.ap, yv.b, ni.b)
                cp("dve", nf.ap, ni.ap.bitcast(I32), ni.b, nf.b)
                tt("dve", nf.ap, yv.ap, nf.ap, SUB, yv.b + nf.b, nf.b)
                act(tab.ap, nf.ap, AF.Sin, nf.b, tab.b, scale=2 * math.pi * (1 - 1e-6))
            AR.release(m2)
            qr = al16([4, TT], name="qr"); kr = al16([4, TT], name="kr")
            m2 = AR.mark()
            for which, dst, scl in ((0, qr, 1.0), (1, kr, 128.0 ** -0.5)):
                w = w_get(); wv = w.ap.rearrange("p (k n) -> p k n", n=512)
                for hh in range(4):
                    k_, pa, pbuf = bank()
                    for kc in range(8):
                        mm(pa[:, 0:TT], wv[:, kc, hh * 128:(hh + 1) * 128], hT.ap[:, kc, :], kc == 0, kc == 7, w.b + hT.b, pbuf)
                    qf = al32([TT], name="qf")
                    cp("act", qf.ap, pa[:, 0:TT], pbuf, qf.b)
                    k2, pr, prb = bank()
                    mm(pr[:, 0:TT], C("rot"), qf.ap, True, True, con.b + qf.b, prb)
                    t1 = al32([TT], name="t1r"); t2 = al32([TT], name="t2r")
                    stt("dve", t1.ap, qf.ap, scl, cos.ap, MUL, MUL, qf.b + cos.b, t1.b)
                    stt("dve", t2.ap, pr[:, 0:TT], scl, sin.ap, MUL, MUL, prb + sin.b, t2.b)
                    tt("dve", dst.ap[:, hh, :], t1.ap, t2.ap, ADD, t1.b + t2.b, dst.b)
                w_release()
            AR.release(m2)
            vt = al16([NTB, 1024], name="vt"); gt = al16([NTB, 1024], name="gt")
            for which, dst in ((0, vt), (1, gt)):
                for half in range(2):
                    w = w_get(); wv = w.ap.rearrange("p (k n) -> p k n", n=512)
                    for tb in range(NTB):
                        k_, pa, pbuf = bank()
                        for kc in range(8):
                            mm(pa, hT.ap[:, kc, tb * 128:(tb + 1) * 128], wv[:, kc, :], kc == 0, kc == 7, w.b + hT.b, pbuf)
                        if which == 0:
                            cp("act", dst.ap[:, tb, half * 512:(half + 1) * 512], pa, pbuf, dst.b)
                        else:
                            act(dst.ap[:, tb, half * 512:(half + 1) * 512], pa, AF.Silu, pbuf, dst.b)
                    w_release()
            for tb in range(NTB):
                m3 = AR.mark()
                cs = slice(tb * 128, (tb + 1) * 128)
                qx = al16([4, 128], name="qx")
                tt("dve", qx.ap, qr.ap[:, :, cs], C("rxi").rearrange("p (h n) -> p h n", n=128), MUL, qr.b + con.b, qx.b)
                k_, pa, pbuf = bank()
                pab = pa.bitcast(BF16)
                for hh in range(4):
                    tr(pab[:, hh * 128:(hh + 1) * 128], kr.ap[:, hh, cs], identb.ap, kr.b + identb.b, pbuf)
                kz = al16([4, 128], name="kz")
                tt("dve", kz.ap, pab[:, 0:512].rearrange("p (h n) -> p h n", n=128), C("rzeta").rearrange("p (h n) -> p h n", n=128), MUL, pbuf + con.b, kz.b)
                k_, pS, pSb = bank()
                for hh in range(4):
                    mm(pS[:, hh * 128:(hh + 1) * 128], kr.ap[:, hh, cs], qr.ap[:, hh, cs], True, True, kr.b + qr.b, pSb)
                ST = al16([4, 128], name="ST")
                tt("dve", ST.ap, pS.rearrange("p (h n) -> p h n", n=128), C("rmask").rearrange("p (h n) -> p h n", n=128), MUL, pSb + con.b, ST.b)
                yb = []
                for pr_ in range(2):
                    k_, py, pyb = bank()
                    for h2 in range(2):
                        hh = pr_ * 2 + h2
                        mm(py[:, h2 * 256:(h2 + 1) * 256], ST.ap[:, hh, :], vt.ap[:, tb, hh * 256:(hh + 1) * 256], True, False, ST.b + vt.b, pyb)
                        mm(py[:, h2 * 256:(h2 + 1) * 256], qx.ap[:, hh, :], Rb.ap[:, hh, :], False, True, qx.b + Rb.b, pyb)
                    yb.append((py, pyb))
                for pr_ in range(2):
                    k_, pR, pRb = bank()
                    for h2 in range(2):
                        hh = pr_ * 2 + h2
                        mm(pR[:, h2 * 256:(h2 + 1) * 256], kz.ap[:, hh, :], vt.ap[:, tb, hh * 256:(hh + 1) * 256], True, True, kz.b + vt.b, pRb)
                    for h2 in range(2):
                        hh = pr_ * 2 + h2
                        stt("dve", Rst.ap[:, hh, :], Rst.ap[:, hh, :], GAM[hh] ** 128, pR[:, h2 * 256:(h2 + 1) * 256], MUL, ADD, Rst.b + pRb, Rst.b)
                cp("act", Rb.ap, Rst.ap, Rst.b, Rb.b)
                yg = al16([1024], name="yg")
                st6 = al32([4, 6], name="st6"); mv = al32([4, 2], name="mv"); rs = al32([4], name="rs")
                for hh in range(4):
                    py, pyb = yb[hh // 2]; ysl = py[:, (hh % 2) * 256:(hh % 2 + 1) * 256]
                    bnst(st6.ap[:, hh, :], ysl, pyb, st6.b)
                    bnag(mv.ap[:, hh, :], st6.ap[:, hh, :], st6.b, mv.b)
                act(rs.ap, mv.ap[:, :, 1], AF.Sqrt, mv.b, rs.b, bias=C("eps_hn"))
                recip(rs.ap, rs.ap, rs.b, rs.b)
                yn = al32([1024], name="yn")
                for hh in range(4):
                    py, pyb = yb[hh // 2]; ysl = py[:, (hh % 2) * 256:(hh % 2 + 1) * 256]
                    ts("dve", yn.ap[:, hh * 256:(hh + 1) * 256], ysl, mv.ap[:, hh, 0:1], rs.ap[:, hh:hh + 1], SUB, MUL, pyb + mv.b + rs.b, yn.b)
                tt("dve", yg.ap, yn.ap, gt.ap[:, tb, :], MUL, yn.b + gt.b, yg.b)
                k_, pT, pTb = bank()
                pTv = pT.bitcast(BF16)
                for c8 in range(8):
                    tr(pTv[:, c8 * 128:(c8 + 1) * 128], yg.ap[:, c8 * 128:(c8 + 1) * 128], identb.ap, yg.b + identb.b, pTb)
                cp("act", yrT.ap[:, :, cs], pTv.rearrange("p (c n) -> p c n", n=128), pTb, yrT.b)
                AR.release(m3)
            AR.release(m1)

        class _Stop(Exception):
            pass

        RSTOP = int(os.environ.get("K_RSTOP", "0"))

        def ck(n):
            if RSTOP == n:
                raise _Stop()

        def rwkv(hT, yrw):
            m9 = AR.mark()
            try:
                rwkv_body(hT, yrw)
            except _Stop:
                AR.release(m9)
                mset(yrw, 0.0)
                while wst["next_use"] % NBLK != 35:
                    w_get(); w_release()

        def rwkv_body(hT, yrw):
            PL = os.environ.get('K_PL', 'pool')
            m1 = AR.mark()
            zq = [al16([8, TT + 1], parts=64, name=f"z{q}") for q in range(3)]
            zdw = al16([TT + 1], parts=64, name="zdw"); zda = al16([TT + 1], parts=64, name="zda"); zdg = al16([TT + 1], name="zdg")
            for q in range(3):
                cp(PL, zq[q].ap[:, :, 0:1], car.ap[:, :, q:q + 1], car.b, zq[q].b)
            cp(PL, zdw.ap[:, 0:1], carl.ap[0:64, 0:1], carl.b, zdw.b)
            cp(PL, zda.ap[:, 0:1], carl.ap[0:64, 1:2], carl.b, zda.b)
            cp(PL, zdg.ap[:, 0:1], carl.ap[:, 2:3], carl.b, zdg.b)
            w = w_get(); wv = w.ap[:, 0:8 * 256].rearrange("p (k n) -> p k n", n=256)
            for (c0, c1, dst) in ((0, 64, zdw), (64, 128, zda), (128, 256, zdg)):
                k_, pa, pbuf = bank(); M = c1 - c0
                for kc in range(8):
                    mm(pa[0:M, 0:TT], wv[:, kc, c0:c1], hT.ap[:, kc, :], kc == 0, kc == 7, w.b + hT.b, pbuf)
                cp("act", dst.ap[:, 1:TT + 1], pa[0:M, 0:TT], pbuf, dst.b)
            w_release()
            for q in range(3):
                w = w_get(); wv = w.ap.rearrange("p (k n) -> p k n", n=512)
                for hh in range(8):
                    k_, pa, pbuf = bank()
                    for kc in range(8):
                        mm(pa[0:64, 0:TT], wv[:, kc, hh * 64:(hh + 1) * 64], hT.ap[:, kc, :], kc == 0, kc == 7, w.b + hT.b, pbuf)
                    cp("act" if hh % 2 else "dve", zq[q].ap[:, hh, 1:TT + 1], pa[0:64, 0:TT], pbuf, zq[q].b)
                w_release()
            ck(1)
            for q in range(3):
                cp(PL, car.ap[:, :, q:q + 1], zq[q].ap[:, :, TT:TT + 1], zq[q].b, car.b)
            cp(PL, carl.ap[0:64, 0:1], zdw.ap[:, TT:TT + 1], zdw.b, carl.b)
            cp(PL, carl.ap[0:64, 1:2], zda.ap[:, TT:TT + 1], zda.b, carl.b)
            cp(PL, carl.ap[:, 2:3], zdg.ap[:, TT:TT + 1], zdg.b, carl.b)

            def bc(nm, n):
                return V(nm)[0:64].unsqueeze(2).to_broadcast([64, 8, n])

            def lerp(eng, dst, z, mu_bc, c0, n, shape3=True):
                zs = z.ap[:, :, c0:c0 + n] if shape3 else z.ap[:, c0:c0 + n]
                zc = z.ap[:, :, c0 + 1:c0 + 1 + n] if shape3 else z.ap[:, c0 + 1:c0 + 1 + n]
                tt(eng, dst.ap, zs, zc, SUB, z.b, dst.b)
                tt(eng, dst.ap, dst.ap, mu_bc, MUL, dst.b + vec.b, dst.b)
                tt(eng, dst.ap, dst.ap, zc, ADD, dst.b + z.b, dst.b)

            for sc in range(NTB):
                m2 = AR.mark()
                c0 = sc * 128
                N8 = 8 * 128
                f2 = lambda t_: t_.ap.rearrange("p h n -> p (h n)")
                ldw = al32([128], parts=64, name="ldw"); lda = al32([128], parts=64, name="lda"); ldg = al32([128], name="ldg")
                lerp(PL, ldw, zdw, V("mu_dw")[0:64].to_broadcast([64, 128]), c0, 128, False)
                lerp(PL, lda, zda, V("mu_da")[0:64].to_broadcast([64, 128]), c0, 128, False)
                lerp(PL, ldg, zdg, V("mu_dg").to_broadcast([128, 128]), c0, 128, False)
                act(tdw.ap[0:64, :], ldw.ap, AF.Tanh, ldw.b, tdw.b)
                cp("dve", tda.ap[0:64, :], lda.ap, lda.b, tda.b)
                sg = al16([128], name="sg")
                act(sg.ap, ldg.ap, AF.Sigmoid, ldg.b, sg.b)
                T1 = al32([8, 128], parts=64, name="T1"); T2 = al32([8, 128], parts=64, name="T2"); T3 = al32([8, 128], parts=64, name="T3")
                T4 = al32([8, 128], parts=64, name="T4"); T5 = al32([8, 128], parts=64, name="T5"); T6 = al32([8, 128], parts=64, name="T6")
                T7 = al32([8, 128], parts=64, name="T7")
                gg = al16([8, 128], parts=64, name="gg")
                for (src, loff, dst, fn, K) in ((tdw, 0, T1, AF.Sigmoid, 65), (tda, 512, T2, AF.Sigmoid, 65), (sg, 1024, gg, AF.Copy, 128)):
                    for half in range(2):
                        k_, pa, pbuf = bank()
                        for h4 in range(4):
                            hh = half * 4 + h4
                            mm(pa[0:64, h4 * 128:(h4 + 1) * 128], lorab.ap[0:K, loff + hh * 64:loff + (hh + 1) * 64], src.ap[0:K, :], True, True, lorab.b + src.b, pbuf)
                        act(f2(dst)[:, half * 512:(half + 1) * 512], pa[0:64, :], fn, pbuf, dst.b)
                ts("dve", f2(T1), f2(T1), -DEC, None, MUL, None, T1.b, T1.b)
                scan(f2(T3), C("reset")[0:64], f2(T1), con.b + T1.b, T3.b)
                tt("dve", f2(T1), f2(T3), f2(T1), SUB, T3.b + T1.b, T1.b)
                act(f2(T1), f2(T1), AF.Exp, T1.b, T1.b)
                act(f2(T4), f2(T3), AF.Exp, T3.b, T4.b)
                act(f2(T3), f2(T3), AF.Exp, T3.b, T3.b, scale=-1.0)
                WC = al32([8, 2], parts=64, name="WC")
                cp("dve", WC.ap, T4.ap.rearrange("p h (c t) -> p h c t", t=64)[:, :, :, 63], T4.b, WC.b)
                for c in range(2):
                    tt("dve", T5.ap[:, :, c * 64:(c + 1) * 64], T3.ap[:, :, c * 64:(c + 1) * 64], WC.ap[:, :, c:c + 1].to_broadcast([64, 8, 64]), MUL, T3.b + WC.b, T5.b)
                ck(2)
                lerp(PL, T6, zq[0], bc("mu_r", 128), c0, 128)
                rt = al16([8, 128], parts=64, name="rt")
                tt("dve", rt.ap, T6.ap, T4.ap, MUL, T6.b + T4.b, rt.b)
                lerp(PL, T4, zq[1], bc("mu_k", 128), c0, 128)
                tt("dve", T7.ap, T4.ap, bc("k_k", 128), MUL, T4.b + vec.b, T7.b)
                sqb = al16([8, 128], parts=64, name="sqb")
                act(sqb.ap, T7.ap, AF.Square, T7.b, sqb.b)
                rn = al32([8, 128], parts=64, name="rn")
                for half in range(2):
                    k_, pa, pbuf = bank()
                    mm(pa[0:64, :], ones64b.ap, f2(sqb)[:, half * 512:(half + 1) * 512], True, True, ones64b.b + sqb.b, pbuf)
                    ts("dve", f2(rn)[:, half * 512:(half + 1) * 512], pa[0:64, :], 1e-24, None, MAXOP, None, pbuf, rn.b)
                act(f2(rn), f2(rn), AF.Sqrt, rn.b, rn.b)
                recip(f2(rn), f2(rn), rn.b, rn.b)
                tt("dve", T7.ap, T7.ap, rn.ap, MUL, T7.b + rn.b, T7.b)
                at = al16([8, 128], parts=64, name="at")
                stt("dve", at.ap, T7.ap, -1.0, T1.ap, MUL, MUL, T7.b + T1.b, at.b)
                tt("dve", T7.ap, T7.ap, T2.ap, MUL, T7.b + T2.b, T7.b)
                tt("dve", T2.ap, T2.ap, bc("k_a", 128), MUL, T2.b + vec.b, T2.b)
                tt("dve", T2.ap, T2.ap, bc("omka", 128), ADD, T2.b + vec.b, T2.b)
                tt("dve", T4.ap, T4.ap, T2.ap, MUL, T4.b + T2.b, T4.b)
                rkr = al16([8, 128], parts=64, name="rkr")
                tt("dve", T1.ap, T6.ap, T4.ap, MUL, T6.b + T4.b, T1.b)
                tt("dve", rkr.ap, T1.ap, bc("r_k", 128), MUL, T1.b + vec.b, rkr.b)
                kt = al16([8, 128], parts=64, name="kt"); bt = al16([8, 128], parts=64, name="bt")
                kh = al16([8, 128], parts=64, name="kh"); bh = al16([8, 128], parts=64, name="bh")
                tt("dve", kt.ap, T4.ap, T3.ap, MUL, T4.b + T3.b, kt.b)
                tt("dve", bt.ap, T7.ap, T3.ap, MUL, T7.b + T3.b, bt.b)
                tt("dve", kh.ap, T4.ap, T5.ap, MUL, T4.b + T5.b, kh.b)
                tt("dve", bh.ap, T7.ap, T5.ap, MUL, T7.b + T5.b, bh.b)
                lerp(PL, T6, zq[2], bc("mu_v", 128), c0, 128)
                vb = al16([8, 128], parts=64, name="vb")
                cp("act", vb.ap, T6.ap, T6.b, vb.b)
                ck(3)
                X = al16([8, 128], name="X"); Vt = al16([8, 64], name="Vt"); Kh = al16([2, 8, 64], name="Kh"); Bh = al16([2, 8, 64], name="Bh")
                mset(Kh, 0.0); mset(Bh, 0.0)
                for (src, dst, w_) in ((at, X, 128), (vb, Vt, 64), (kh, Kh, 0), (bh, Bh, 0)):
                    k_, pa, pbuf = bank(); pv = pa.bitcast(BF16)[:, 0:512].rearrange("p (h k) -> p h k", k=64)
                    for hh in range(8):
                        tr(pv[:, hh, :], src.ap[:, hh, :], identb.ap[0:64, 0:64], src.b + identb.b, pbuf)
                    if w_:
                        cp("act", dst.ap[:, :, 0:64], pv, pbuf, dst.b)
                    else:
                        cp("act", dst.ap[0:64, 0, :, :], pv[0:64], pbuf, dst.b)
                        cp("dve", dst.ap[64:128, 1, :, :], pv[64:128], pbuf, dst.b)
                ck(4)
                sc4 = al16([8, 4, 128], name="sc4")
                Lab = al16([8, 128], name="Lab")
                for hh in range(8):
                    k_, pa, pbuf = bank()
                    mm(pa[:, 0:128], bt.ap[:, hh, :], at.ap[:, hh, :], True, True, bt.b + at.b, pbuf)
                    mm(pa[:, 128:256], bt.ap[:, hh, :], rt.ap[:, hh, :], True, True, bt.b + rt.b, pbuf)
                    mm(pa[:, 256:384], kt.ap[:, hh, :], at.ap[:, hh, :], True, True, kt.b + at.b, pbuf)
                    mm(pa[:, 384:512], kt.ap[:, hh, :], rt.ap[:, hh, :], True, True, kt.b + rt.b, pbuf)
                    tt("dve", sc4.ap[:, hh, :, :], pa.rearrange("p (f t) -> p f t", t=128), C("mask4").rearrange("p (f t) -> p f t", t=128), MUL, pbuf + con.b, sc4.b)
                for half in range(2):
                    k_, pa, pbuf = bank()
                    for h4 in range(4):
                        hh = half * 4 + h4
                        mm(pa[:, h4 * 128:(h4 + 1) * 128], at.ap[:, hh, :], bt.ap[:, hh, :], True, True, at.b + bt.b, pbuf)
                    tt("dve", Lab.ap[:, half * 4:(half + 1) * 4, :], pa.rearrange("p (f t) -> p f t", t=128), C("msl4").rearrange("p (f t) -> p f t", t=128), MUL, pbuf + con.b, Lab.b)
                for half in range(2):
                    k_, pa, pbuf = bank()
                    for h4 in range(4):
                        hh = half * 4 + h4
                        mm(pa[:, h4 * 64:(h4 + 1) * 64], sc4.ap[:, hh, 2, :], Vt.ap[:, hh, :], True, True, sc4.b + Vt.b, pbuf)
                    cp("act", X.ap[:, half * 4:(half + 1) * 4, 64:128], pa[:, 0:256].rearrange("p (h v) -> p h v", v=64), pbuf, X.b)
                ck(5)
                A = Lab
                A2s = [al16([8, 2, 128], name="A2a"), al16([8, 2, 128], name="A2b")]
                AT_ap = lambda hh: sc4.ap[:, hh, 0, :]
                AT_b = sc4.b
                for lvl in range(6):
                    for half in range(2):
                        k_, pa, pbuf = bank()
                        for h4 in range(4):
                            hh = half * 4 + h4
                            mm(pa[:, h4 * 128:(h4 + 1) * 128], AT_ap(hh), X.ap[:, hh, :], True, True, AT_b + X.b, pbuf)
                        tt("dve", X.ap[:, half * 4:(half + 1) * 4, :], X.ap[:, half * 4:(half + 1) * 4, :], pa.rearrange("p (h v) -> p h v", v=128), ADD, X.b + pbuf, X.b)
                    if lvl == 5:
                        break
                    A2 = A2s[lvl % 2]
                    for hh in range(8):
                        k_, pa, pbuf = bank()
                        a_ap = A.ap[:, hh, :] if lvl == 0 else A.ap[:, hh, 0, :]
                        mm(pa[:, 0:128], AT_ap(hh), a_ap, True, True, AT_b + A.b, pbuf)
                        mm(pa[:, 128:256], a_ap, AT_ap(hh), True, True, AT_b + A.b, pbuf)
                        cp("act" if hh % 2 else "dve", A2.ap[:, hh, :, :], pa[:, 0:256].rearrange("p (f t) -> p f t", t=128), pbuf, A2.b)
                    A = A2
                    AT_ap = (lambda A2: lambda hh: A2.ap[:, hh, 1, :])(A2)
                    AT_b = A2.b
                ck(6)
                PQ = Tile(f2(T7).bitcast(BF16).rearrange("p (h t) -> p h t", t=256), T7.bufs)
                GH = al32([8, 256], parts=64, name="GH")
                for hh in range(8):
                    k_, pa, pbuf = bank()
                    mm(pa[0:64, 0:128], X.ap[:, hh, 0:64], sc4.ap[:, hh, 1, :], True, False, X.b + sc4.b, pbuf)
                    mm(pa[0:64, 0:128], identb.ap[0:64, 0:64], rt.ap[:, hh, :], False, True, identb.b + rt.b, pbuf)
                    mm(pa[0:64, 128:256], X.ap[:, hh, 64:128], sc4.ap[:, hh, 1, :], True, False, X.b + sc4.b, pbuf)
                    mm(pa[0:64, 128:256], Vt.ap[:, hh, :], sc4.ap[:, hh, 3, :], False, True, Vt.b + sc4.b, pbuf)
                    for c in range(2):
                        o = 256 + c * 128
                        mm(pa[0:64, o:o + 64], X.ap[:, hh, 0:64], Bh.ap[:, c, hh, :], True, True, X.b + Bh.b, pbuf)
                        mm(pa[0:64, o + 64:o + 128], Bh.ap[:, c, hh, :], X.ap[:, hh, 64:128], True, False, X.b + Bh.b, pbuf)
                        mm(pa[0:64, o + 64:o + 128], Kh.ap[:, c, hh, :], Vt.ap[:, hh, :], False, True, Kh.b + Vt.b, pbuf)
                    cp("dve", PQ.ap[:, hh, :], pa[0:64, 0:256], pbuf, PQ.b)
                    cp("dve", GH.ap[:, hh, :], pa[0:64, 256:512], pbuf, GH.b)
                ck(7)
                Sb = Tile(f2(sqb)[:, 0:512].rearrange("p (h v) -> p h v", v=64), sqb.bufs)
                yv_ = T4
                for c in range(2):
                    cp("act", Sb.ap, Sst.ap, Sst.b, Sb.b)
                    k_, pa, pbuf = bank()
                    for hh in range(8):
                        mm(pa[0:64, hh * 64:(hh + 1) * 64], Sb.ap[:, hh, :], PQ.ap[:, hh, c * 64:(c + 1) * 64], True, False, Sb.b + PQ.b, pbuf)
                        mm(pa[0:64, hh * 64:(hh + 1) * 64], identb.ap[0:64, 0:64], PQ.ap[:, hh, 128 + c * 64:128 + (c + 1) * 64], False, True, identb.b + PQ.b, pbuf)
                    cp("act", yv_.ap[:, :, c * 64:(c + 1) * 64], pa[0:64, :].rearrange("p (h t) -> p h t", t=64), pbuf, yv_.b)
                    k_, pb_, pbb = bank()
                    for hh in range(8):
                        mm(pb_[0:64, hh * 64:(hh + 1) * 64], GH.ap[:, hh, c * 128:c * 128 + 64], Sst.ap[:, hh, :], True, False, GH.b + Sst.b, pbb)
                        mm(pb_[0:64, hh * 64:(hh + 1) * 64], C("ident")[0:64, 0:64], GH.ap[:, hh, c * 128 + 64:c * 128 + 128], False, True, con.b + GH.b, pbb)
                    tt("dve", Sst.ap, Sst.ap, WC.ap[:, :, c:c + 1].to_broadcast([64, 8, 64]), MUL, Sst.b + WC.b, Sst.b)
                    tt("dve", Sst.ap, Sst.ap, pb_[0:64, :].rearrange("p (h v) -> p h v", v=64), ADD, Sst.b + pbb, Sst.b)
                ck(8)
                ysq = T1; mean = T2; var = T3; bon = T5
                act(f2(ysq), f2(yv_), AF.Square, yv_.b, ysq.b)
                for half in range(2):
                    hs = slice(half * 512, (half + 1) * 512)
                    k_, pm, pmb = bank(); k2, pq, pqb = bank(); k3, pbn, pbnb = bank()
                    mm(pm[0:64, :], C("ones64m")[0:64], f2(yv_)[:, hs], True, True, con.b + yv_.b, pmb)
                    mm(pq[0:64, :], C("ones64m")[0:64], f2(ysq)[:, hs], True, True, con.b + ysq.b, pqb)
                    mm(pbn[0:64, :], ones64b.ap, f2(rkr)[:, hs], True, True, ones64b.b + rkr.b, pbnb)
                    cp("act", f2(mean)[:, hs], pm[0:64, :], pmb, mean.b)
                    tt("dve", f2(var)[:, hs], f2(mean)[:, hs], f2(mean)[:, hs], MUL, mean.b, var.b)
                    tt("dve", f2(var)[:, hs], pq[0:64, :], f2(var)[:, hs], SUB, pqb + var.b, var.b)
                    tt("dve", f2(bon)[:, hs], pbn[0:64, :], f2(T6)[:, hs], MUL, pbnb + T6.b, bon.b)
                act(f2(var), f2(var), AF.Sqrt, var.b, var.b, bias=C("eps_gn")[0:64])
                recip(f2(var), f2(var), var.b, var.b)
                tt("dve", yv_.ap, yv_.ap, mean.ap, SUB, yv_.b + mean.b, yv_.b)
                tt("dve", yv_.ap, yv_.ap, var.ap, MUL, yv_.b + var.b, yv_.b)
                tt("dve", yv_.ap, yv_.ap, bc("gn_g", 128), MUL, yv_.b + vec.b, yv_.b)
                tt("dve", yv_.ap, yv_.ap, bc("gn_b", 128), ADD, yv_.b + vec.b, yv_.b)
                tt("dve", yv_.ap, yv_.ap, bon.ap, ADD, yv_.b + bon.b, yv_.b)
                tt("dve", yrw.ap[:, :, c0:c0 + 128], yv_.ap, gg.ap, MUL, yv_.b + gg.b, yrw.b)
                AR.release(m2)
            AR.release(m1)

        def merge_out(h, hT, yrw, yrT):
            m1 = AR.mark()
            mT = al16([8, TT], name="mT")
            sps = [(al32([TT], name="s1"), al32([TT], name="s2")) for _ in range(2)]
            for hb in range(2):
                wbr = w_get(); wbt = w_get(); wg = [w_get(), w_get()]
                wbrv = wbr.ap[0:64].rearrange("p (k n) -> p k n", n=512)
                wbtv = wbt.ap.rearrange("p (k n) -> p k n", n=512)
                for j4 in range(4):
                    j = hb * 4 + j4
                    wgv = wg[j4 // 2].ap.rearrange("p (k n) -> p k n", n=512); jo = (j4 % 2) * 128
                    k_, pgr, pgrb = bank(); k_, pgt, pgtb = bank(); k_, pa, pab_ = bank(); k_, pb_, pbb = bank()
                    for kc in range(8):
                        mm(pgr[:, 0:TT], wgv[:, kc, jo:jo + 128], hT.ap[:, kc, :], kc == 0, kc == 7, wg[j4 // 2].b + hT.b, pgrb)
                    for kc in range(8):
                        mm(pgt[:, 0:TT], wgv[:, kc, 256 + jo:256 + jo + 128], hT.ap[:, kc, :], kc == 0, kc == 7, wg[j4 // 2].b + hT.b, pgtb)
                    for kc in range(8):
                        mm(pa[:, 0:TT], wbrv[:, kc, j4 * 128:(j4 + 1) * 128], yrw.ap[:, kc, :], kc == 0, kc == 7, wbr.b + yrw.b, pab_)
                    for kc in range(8):
                        mm(pb_[:, 0:TT], wbtv[:, kc, j4 * 128:(j4 + 1) * 128], yrT.ap[:, kc, :], kc == 0, kc == 7, wbt.b + yrT.b, pbb)
                    s1, s2 = sps[j % 2]
                    act(s1.ap, pgr[:, 0:TT], AF.Sigmoid, pgrb, s1.b)
                    act(s2.ap, pgt[:, 0:TT], AF.Sigmoid, pgtb, s2.b)
                    tt("dve", s1.ap, pa[:, 0:TT], s1.ap, MUL, pab_ + s1.b, s1.b)
                    tt("dve", s2.ap, pb_[:, 0:TT], s2.ap, MUL, pbb + s2.b, s2.b)
                    tt("dve", mT.ap[:, j, :], s1.ap, s2.ap, ADD, s1.b + s2.b, mT.b)
                w_release()
            for hb in range(2):
                w = w_get(); wv = w.ap.rearrange("p (k n) -> p k n", n=512)
                for j4 in range(4):
                    j = hb * 4 + j4
                    k_, pa, pbuf = bank()
                    for kc in range(8):
                        mm(pa[:, 0:TT], wv[:, kc, j4 * 128:(j4 + 1) * 128], mT.ap[:, kc, :], kc == 0, kc == 7, w.b + mT.b, pbuf)
                    stt("dve", h.ap[:, j, :], pa[:, 0:TT], 1.0 / ALPHA, h.ap[:, j, :], MUL, ADD, pbuf + h.b, h.b)
                w_release()
            AR.release(m1)

        for ti in range(n_tiles):
            t0 = ti * TT
            mt = AR.mark()
            h = al32([8, TT], name="h"); hT = al16([8, TT], name="hT")
            def tile_loads(tj):
                tq = tj * TT
                xq, pq_, oq = xts[tj % NXB], pts[tj % NXB], posbs[tj % NXB]
                dma("sp", xq.ap, x_d[tq:tq + TT, :].rearrange("(b p) d -> p b d", p=128), [], xq.b)
                dma("sp", pq_.ap, p_d[tq:tq + TT, :].rearrange("(b p) d -> p b d", p=128), [], pq_.b)
                dma("sp", oq.ap.bitcast(I32), pos_d[:, tq:tq + TT].partition_broadcast(128), [], oq.b)
            if NXB == 1:
                tile_loads(ti)
            else:
                if ti == 0:
                    tile_loads(0)
                if ti + 1 < n_tiles:
                    tile_loads(ti + 1)
            xt, pt, posb = xts[ti % NXB], pts[ti % NXB], posbs[ti % NXB]
            for j in range(8):
                k_, pa, pbuf = bank()
                for tb in range(NTB):
                    tr(pa[:, tb * 128:(tb + 1) * 128], xt.ap[:, tb, j * 128:(j + 1) * 128], C("ident"), xt.b + con.b, pbuf)
                cp("act", h.ap[:, j, :], pa[:, 0:TT], pbuf, h.b)
                cp("dve", hT.ap[:, j, :], pa[:, 0:TT], pbuf, hT.b)
            if stop >= 1:
                ffn_ln(h, hT, "ln1_g", "ln1_b", 0.5 / ALPHA)
            if stop >= 2:
                yrT = al16([8, TT], name="yrT"); yrw = al16([8, TT], parts=64, name="yrw")
                MIX = os.environ.get("K_MIX", "both")
                if MIX in ("both", "ret"):
                    retention(hT, yrT, ti)
                else:
                    mset(yrT, 0.0)
                    for _ in range(6):
                        w_get(); w_release()
                if MIX in ("both", "rwkv"):
                    rwkv(hT, yrw)
                else:
                    mset(yrw, 0.0)
                    for _ in range(4):
                        w_get(); w_release()
                merge_out(h, hT, yrw, yrT)
                layer_norm(h, hT, "ln2_g", "ln2_b")
            if stop >= 3:
                ffn_ln(h, hT, "ln3_g", "ln3_b", 0.5 / ALPHA)
            if stop >= 4:
                pT = al16([2, TT], name="pT")
                for kc in range(2):
                    k_, pa, pbuf = bank()
                    for tb in range(NTB):
                        tr(pa[:, tb * 128:(tb + 1) * 128], pt.ap[:, tb, kc * 128:(kc + 1) * 128], C("ident"), pt.b + con.b, pbuf)
                    cp("act", pT.ap[:, kc, :], pa[:, 0:TT], pbuf, pT.b)
                wg0 = w_get(); wg1 = w_get(); wp = w_get()
                wpv = wp.ap[:, 0:2048].rearrange("p (k n) -> p k n", n=1024)
                for j in range(8):
                    wgx = (wg0, wg1)[j // 4]; wgv = wgx.ap.rearrange("p (k n) -> p k n", n=512); j4 = j % 4
                    k_, pa, pbuf = bank(); k_, pp, ppb = bank()
                    for kc in range(8):
                        mm(pa[:, 0:TT], wgv[:, kc, j4 * 128:(j4 + 1) * 128], hT.ap[:, kc, :], kc == 0, kc == 7, wgx.b + hT.b, pbuf)
                    for kc in range(2):
                        mm(pp[:, 0:TT], wpv[:, kc, j * 128:(j + 1) * 128], pT.ap[:, kc, :], kc == 0, kc == 1, wp.b + pT.b, ppb)
                    sgt = al32([TT], name="sgt")
                    act(sgt.ap, pa[:, 0:TT], AF.Sigmoid, pbuf, sgt.b)
                    tt("dve", sgt.ap, pp[:, 0:TT], sgt.ap, MUL, ppb + sgt.b, sgt.b)
                    tt("dve", h.ap[:, j, :], h.ap[:, j, :], sgt.ap, ADD, h.b + sgt.b, h.b)
                w_release()
            while wst["next_use"] < min((ti + 1) * NBLK, total_blocks - NSLOT) and not NOPRO:
                w_get(); w_release()
            ot = al32([NTB, D], name="ot")
            for tb in range(NTB):
                for half in range(2):
                    k_, pa, pbuf = bank()
                    for j4 in range(4):
                        j = half * 4 + j4
                        tr(pa[:, j4 * 128:(j4 + 1) * 128], h.ap[:, j, tb * 128:(tb + 1) * 128], C("ident"), h.b + con.b, pbuf)
                    cp("act" if half else "dve", ot.ap[:, tb, half * 512:(half + 1) * 512], pa, pbuf, ot.b)
            if not (ti > 0 and os.environ.get("K_SKIPST1") == "1"):
              dma(os.environ.get("K_STQ", "act"), out_d[t0:t0 + TT, :].rearrange("(b p) d -> p b d", p=128), ot.ap, ot.b, [])
            AR.release(mt)
        fin = []
        for (o, e, acc) in AR.hist:
            fin.extend(acc)
        fb = Buf("fin"); fb.r = fin
        S.add("pool", None, reads=[], writes=[fb])
        lastids = {}
        for d_ in range(S.ndma):
            lastids[d_ % NPOOL] = d_
        fb2 = Buf("fin2"); fb2.r = [("d", d_) for d_ in lastids.values()]
        S.add("pool", None, reads=[], writes=[fb2])
        S.emit(nc, block, sems, dsems)
        print("arena peak words", AR.peak, "of", NAR, "ops", {e: len(S.ops[e]) for e in S.ENGS})
    return nc


_NC_CACHE = {}


def kernel(**inputs):
    n_tiles = SEQ // TT
    wall, V, lora = host_prep(inputs)
    consts = make_consts()
    x = np.ascontiguousarray(np.asarray(inputs["x"], np.float32))
    p = np.ascontiguousarray(np.asarray(inputs["p"], np.float32))[0]
    pos = np.ascontiguousarray(np.asarray(inputs["positions"], np.int32))
    if "nc" not in _NC_CACHE:
        _NC_CACHE["nc"] = build(n_tiles)
    nc = _NC_CACHE["nc"]
    in_maps = []
    for b in range(NB_):
        in_maps.append({"x": x[b], "p": p[b], "pos": pos[b:b + 1], "wall": wall, "vecs": V, "consts": consts, "lora": lora})
    res = run_bass_kernel_spmd(nc, in_maps, core_ids=list(range(NB_)))
    return np.stack([np.asarray(res.results[b]["out"], np.float32) for b in range(NB_)], axis=0)
```
